# Optimizing a Trainium2 kernel written in Bass

```python
import math
import jax, jax.numpy as jnp
from jax import lax
import numpy as np

D_MODEL = 2048
BATCH = 4
SEQ = 2048
DEPTH = 4

CHUNK = 64
Q_BLOCK = 128
HEAD_DIM = 128
D_MIX = D_MODEL
D_SB = D_MIX // 2
D_CH = D_MIX - D_SB
N_HEADS_SB = D_SB // HEAD_DIM
N_HEADS_CH = D_CH // HEAD_DIM
LEFT_CHUNKS = 8
BAND = LEFT_CHUNKS + 1
REL_CLIP = 256
N_REL = REL_CLIP + CHUNK
D_IN = 4 * D_SB + 4 * D_CH
NORM_EPS = 1e-6
NEG_BIG = -1e30

kernel_name = "hybrid_stickbreak_chunkband_trunk"


def rms_norm(x, g):
    xf = x.astype(jnp.float32)
    y = xf * lax.rsqrt(jnp.mean(xf * xf, axis=-1, keepdims=True) + NORM_EPS)
    return (y * g.astype(jnp.float32)).astype(x.dtype)


def split_heads(t, n_heads):
    b, s, _ = t.shape
    return t.reshape(b, s, n_heads, HEAD_DIM).transpose(0, 2, 1, 3)


def merge_heads(t):
    b, h, s, d = t.shape
    return t.transpose(0, 2, 1, 3).reshape(b, s, h * d)


def stick_breaking_attention(q, k, v):
    seq = q.shape[2]
    scale = q.shape[-1] ** -0.5
    outs = []
    for blk in range(seq // Q_BLOCK):
        t0 = blk * Q_BLOCK
        t1 = t0 + Q_BLOCK
        qb = q[:, :, t0:t1]
        kb = k[:, :, :t1]
        vb = v[:, :, :t1]
        z = jnp.einsum('bhtd,bhsd->bhts', qb, kb).astype(jnp.float32) * scale
        t_idx = jnp.arange(t0, t1)[:, None]
        s_idx = jnp.arange(t1)[None, :]
        causal = s_idx < t_idx
        log_stay = jnp.where(causal, jax.nn.log_sigmoid(-z), 0.0)
        after = lax.cumsum(log_stay, axis=3, reverse=True) - log_stay
        w = jnp.where(causal, jnp.exp(jax.nn.log_sigmoid(z) + after), 0.0)
        outs.append(jnp.einsum('bhts,bhsd->bhtd', w.astype(v.dtype), vb))
    return jnp.concatenate(outs, axis=2)


def chunk_band_attention(q, k, v, q_gain, k_gain, rel_table):
    b, h, seq, d = q.shape
    nc = seq // CHUNK
    q = rms_norm(q, q_gain)
    k = rms_norm(k, k_gain)
    qc = q.reshape(b, h, nc, CHUNK, d)
    pad = ((0, 0), (0, 0), (LEFT_CHUNKS * CHUNK, 0), (0, 0))
    kc = jnp.pad(k, pad).reshape(b, h, nc + LEFT_CHUNKS, CHUNK, d)
    vc = jnp.pad(v, pad).reshape(b, h, nc + LEFT_CHUNKS, CHUNK, d)
    band_idx = jnp.arange(nc)[:, None] + jnp.arange(BAND)[None, :]
    kband = kc[:, :, band_idx].reshape(b, h, nc, BAND * CHUNK, d)
    vband = vc[:, :, band_idx].reshape(b, h, nc, BAND * CHUNK, d)
    scores = jnp.einsum('bhcid,bhcpd->bhcip', qc, kband).astype(jnp.float32) * (d ** -0.5)
    i_pos = np.arange(CHUNK)[:, None]
    p_pos = np.arange(BAND * CHUNK)[None, :]
    dist = LEFT_CHUNKS * CHUNK + i_pos - p_pos
    rel_idx = np.clip(dist, -(CHUNK - 1), REL_CLIP) + (CHUNK - 1)
    bias = rel_table.astype(jnp.float32)[:, rel_idx]
    scores = scores + bias[None, :, None]
    valid = jnp.repeat(band_idx >= LEFT_CHUNKS, CHUNK, axis=1)
    scores = jnp.where(valid[None, None, :, None, :], scores, NEG_BIG)
    probs = jax.nn.softmax(scores, axis=-1)
    out = jnp.einsum('bhcip,bhcpd->bhcid', probs.astype(v.dtype), vband)
    return out.reshape(b, h, seq, d)


def setup_inputs(seed: int = 0) -> dict:
    key = jax.random.key(seed)
    ks = jax.random.split(key, 8)
    x = jax.random.normal(ks[0], (BATCH, SEQ, D_MODEL), jnp.float32)
    norm_g = 1.0 + 0.02 * jax.random.normal(ks[1], (DEPTH, D_MODEL), jnp.float32)
    w_in = jax.random.normal(ks[2], (DEPTH, D_MODEL, D_IN), jnp.float32) * D_MODEL ** -0.5
    q_norm_g = 1.0 + 0.02 * jax.random.normal(ks[3], (DEPTH, HEAD_DIM), jnp.float32)
    k_norm_g = 1.0 + 0.02 * jax.random.normal(ks[4], (DEPTH, HEAD_DIM), jnp.float32)
    rel_bias = 0.1 * jax.random.normal(ks[5], (DEPTH, N_HEADS_CH, N_REL), jnp.float32)
    w_out = jax.random.normal(ks[6], (DEPTH, D_MIX, D_MODEL), jnp.float32) * D_MIX ** -0.5
    return {"x": x, "norm_g": norm_g, "w_in": w_in, "q_norm_g": q_norm_g,
            "k_norm_g": k_norm_g, "rel_bias": rel_bias, "w_out": w_out}


def reference(x, norm_g, w_in, q_norm_g, k_norm_g, rel_bias, w_out):
    splits = np.cumsum([D_SB, D_SB, D_SB, D_SB, D_CH, D_CH, D_CH])
    for layer in range(DEPTH):
        h = rms_norm(x, norm_g[layer])
        proj = jnp.einsum('bsd,de->bse', h, w_in[layer])
        qa, ka, va, ga, qb, kb, vb, gb = jnp.split(proj, splits, axis=-1)
        ya = stick_breaking_attention(split_heads(qa, N_HEADS_SB),
                                      split_heads(ka, N_HEADS_SB),
                                      split_heads(va, N_HEADS_SB))
        yb = chunk_band_attention(split_heads(qb, N_HEADS_CH),
                                  split_heads(kb, N_HEADS_CH),
                                  split_heads(vb, N_HEADS_CH),
                                  q_norm_g[layer], k_norm_g[layer], rel_bias[layer])
        mixed = jnp.concatenate([merge_heads(ya) * jax.nn.silu(ga),
                                 merge_heads(yb) * jax.nn.silu(gb)], axis=-1)
        x = x + jnp.einsum('bse,ed->bsd', mixed, w_out[layer])
    return x
```

```python
import contextlib
import numpy as np
import ml_dtypes
import concourse.bass as bass
import concourse.mybir as mybir
from concourse.bass_utils import run_bass_kernel_spmd

F32 = mybir.dt.float32
BF16 = mybir.dt.bfloat16
AF = mybir.ActivationFunctionType
ALU = mybir.AluOpType

D = 2048
T = 2048
NT = T // 128
DC = D // 128
DEPTH = 4
NHA = 8
NHB = 8
HD = 128
EPS = 1e-6
NEG = -30000.0
SAME_ENG_SYNC = True
N_CORES = 8
TO = T // 2
NTO = TO // 128
NHA_C = NHA // 2
NHB_C = NHB // 2


class Tile:
    __slots__ = ("name", "w", "r", "excl")

    def __init__(self, name, excl=False):
        self.name = name
        self.w = None
        self.r = {}
        self.excl = excl


class Op:
    __slots__ = ("eng", "fn", "deps", "needed", "sig", "dsem", "dval", "is_dma", "cc", "ph")

    def __init__(self, eng, fn):
        self.eng = eng
        self.fn = fn
        self.deps = []
        self.needed = False
        self.sig = 0
        self.dsem = None
        self.dval = 0
        self.is_dma = False
        self.cc = False
        self.ph = ""


class Prog:
    ENGS = ("pe", "act", "dve", "pool", "sp")

    def __init__(self):
        self.ops = {e: [] for e in self.ENGS}
        self.dma_cnt = {}
        self.phase = ""

    def op(self, eng, fn, reads=(), writes=(), dsem=None, cc=False):
        o = Op(eng, fn)
        o.ph = self.phase
        deps = {}
        for t in reads:
            if t.w is not None:
                deps[id(t.w)] = t.w
            if t.excl:
                for k, v in t.r.items():
                    if k != eng and not isinstance(v, list):
                        deps[id(v)] = v
        for t in writes:
            if t.w is not None:
                deps[id(t.w)] = t.w
            for k, v in t.r.items():
                if isinstance(v, list):
                    for d in v:
                        deps[id(d)] = d
                else:
                    deps[id(v)] = v
        o.deps = list(deps.values())
        if dsem is not None:
            o.is_dma = True
            o.dsem = dsem
            inc = 1 if cc else 16
            self.dma_cnt[dsem] = self.dma_cnt.get(dsem, 0) + inc
            o.dval = self.dma_cnt[dsem]
            o.cc = cc
        for t in reads:
            if o.is_dma:
                t.r.setdefault("dma", []).append(o)
            else:
                t.r[eng] = o
        for t in writes:
            t.w = o
            t.r = {}
        self.ops[eng].append(o)
        return o

    def _skip(self, o, d):
        return (not d.is_dma) and (not o.is_dma) and d.eng == o.eng and (d.eng == "pe" or not SAME_ENG_SYNC)

    def emit(self, block, sems, dsems):
        for e in self.ENGS:
            for o in self.ops[e]:
                for d in o.deps:
                    if d.is_dma or self._skip(o, d):
                        continue
                    d.needed = True
        for e in self.ENGS:
            c = 0
            for o in self.ops[e]:
                if o.needed and not o.is_dma:
                    c += 1
                    o.sig = c
        self.sig_counts = {e: sum(1 for o in self.ops[e] if o.needed and not o.is_dma) for e in self.ENGS}
        self.nwaits = {e: 0 for e in self.ENGS}

        def run(engname):
            def body(eng):
                waited = {}
                for o in self.ops[engname]:
                    need = {}
                    for d in o.deps:
                        if d.is_dma:
                            key, sem, val = ("d", d.dsem), dsems[d.dsem], d.dval
                        else:
                            if self._skip(o, d):
                                continue
                            key, sem, val = ("e", d.eng), sems[d.eng], d.sig
                        if key not in need or need[key][1] < val:
                            need[key] = (sem, val)
                    for key, (sem, val) in need.items():
                        if waited.get(key, 0) >= val:
                            continue
                        waited[key] = val
                        eng.wait_ge(sem, val)
                        self.nwaits[engname] += 1
                    inst = o.fn(eng)
                    if o.cc:
                        inst.then_inc(dsems[o.dsem])
                    elif o.is_dma:
                        inst.then_inc(dsems[o.dsem], 16)
                    elif o.needed:
                        inst.then_inc(sems[engname], 1)
                fin = {}
                for o in self.ops[engname]:
                    if o.is_dma:
                        fin[o.dsem] = max(fin.get(o.dsem, 0), o.dval)
                for k, v in fin.items():
                    eng.wait_ge(dsems[k], v)
            return body

        block.tensor(run("pe"))
        block.scalar(run("act"))
        block.vector(run("dve"))
        block.gpsimd(run("pool"))
        block.sync(run("sp"))


C_IDENT, C_TRI, C_ONES, C_ODIV, C_NEGA = 0, 128, 256, 384, 512
C_EP = 640
C_NEGW = 640 + 16 * 48
C_TOT = C_NEGW + 2048


def host_consts():
    c = np.zeros((128, C_TOT), np.float32)
    c[:, C_IDENT:C_IDENT + 128] = np.eye(128)
    sp_, s_ = np.meshgrid(np.arange(128), np.arange(128), indexing="ij")
    c[:, C_TRI:C_TRI + 128] = np.where(sp_ >= s_, -1.0, 0.0)
    c[:, C_ONES:C_ONES + 128] = 1.0
    c[:, C_ODIV:C_ODIV + 128] = 1.0 / 128.0
    s_i, t_i = np.meshgrid(np.arange(128), np.arange(128), indexing="ij")
    c[:, C_NEGA:C_NEGA + 128] = np.where(s_i < t_i, 0.0, NEG)
    trm = np.where(s_i < t_i, 0.0, NEG)
    for jd in range(4):
        for b in range(4):
            c0 = C_NEGW + jd * 512 + b * 128
            if b < jd:
                c[:, c0:c0 + 128] = NEG
            elif b == jd:
                c[:, c0:c0 + 128] = trm
    for j in range(16):
        c[:, C_EP + j * 48 + j] = 1.0
        c[:, C_EP + j * 48 + 32 + j] = 1.0
    lsel = np.zeros((48, 16 * 128), np.float32)
    for j in range(16):
        for k in range(48):
            if (k < 16 or k >= 32) and (k % 32) > j:
                lsel[k, j * 128:(j + 1) * 128] = -1.0
    maskb = np.zeros((128, 5, 128), np.float32)
    maskb[:64, 4, 64:] = NEG
    maskb[64:, 0, :64] = NEG
    return (c.astype(ml_dtypes.bfloat16), lsel.astype(ml_dtypes.bfloat16),
            maskb.reshape(128, 640).astype(np.float32))


def build_program(depth=DEPTH, n_ha=NHA_C, n_hb=NHB_C, dbg=9, n_cores=N_CORES):
    nc = bass.Bass("TRN2", target_bir_lowering=False)
    NH = n_ha + n_hb
    groups = []
    if n_ha:
        groups.append(list(range(0, n_ha)))
    if n_hb:
        groups.append(list(range(n_ha, NH)))
    NG = len(groups)
    GMAX = max(len(g) for g in groups)
    SC = float(HD ** -0.5)

    x_d = nc.dram_tensor("x", [TO, D], F32, kind="ExternalInput").ap()
    y_d = nc.dram_tensor("y", [TO, D], F32, kind="ExternalOutput").ap()
    hsel_d = nc.dram_tensor("hsel", [128, 2], F32, kind="ExternalInput").ap()
    hx_in = [nc.dram_tensor(f"hx_in{i}", [D, 512], BF16) for i in range(2)]
    hx_out = [nc.dram_tensor(f"hx_out{i}", [2 * D, 512], BF16) for i in range(2)]
    mx_in = [nc.dram_tensor(f"mx_in{i}", [128, T], BF16) for i in range(2)]
    mx_out = [nc.dram_tensor(f"mx_out{i}", [256, T], BF16) for i in range(2)]
    RG = [[2 * i, 2 * i + 1] for i in range(n_cores // 2)]
    win_d = nc.dram_tensor("win", [depth * NH * 2 * 128, 16 * 256], F32, kind="ExternalInput").ap()
    wout_d = nc.dram_tensor("wout", [depth * NG * 4 * 128, 2 * GMAX * 512], F32, kind="ExternalInput").ap()
    ng_d = nc.dram_tensor("ng", [depth, D], F32, kind="ExternalInput").ap()
    gains_d = nc.dram_tensor("gains", [128, 2 * depth], F32, kind="ExternalInput").ap()
    bias_d = nc.dram_tensor("biasT", [depth * max(n_hb, 1) * 128, 640], F32, kind="ExternalInput").ap()
    maskb_d = nc.dram_tensor("maskb", [128, 640], F32, kind="ExternalInput").ap()
    cbf_d = nc.dram_tensor("cbf", [128, C_TOT], BF16, kind="ExternalInput").ap()
    lsel_d = nc.dram_tensor("lsel", [48, 2048], BF16, kind="ExternalInput").ap()

    P = Prog()
    es = contextlib.ExitStack()

    def sb(name, shape, dt):
        return es.enter_context(nc.sbuf_tensor(name, shape, dt))

    def ps(name, shape, dt):
        return es.enter_context(nc.psum_tensor(name, shape, dt))

    with es:
        hT = sb("hT", [128, DC, T], BF16)
        mixT = sb("mixT", [128, 2, T], BF16)
        mixSel = sb("mixSel", [128, 2 * GMAX, TO], BF16)
        stg_lo = sb("stg_lo", [128, 2, TO], BF16)
        stg_hi = sb("stg_hi", [128, 2, TO], BF16)
        hsel = sb("hsel_s", [128, 2], F32)
        wh = [sb(f"wh{i}", [128, 16, 256], BF16) for i in range(3)]
        qT = sb("qT", [128, T], BF16)
        kT = sb("kT", [128, T], BF16)
        sgT = sb("sgT", [128, T], BF16)
        Vt = sb("Vt", [128, NT, 128], BF16)
        f512 = [sb(f"f512_{i}", [128, 512], F32) for i in range(4)]
        b512 = [sb(f"b512_{i}", [128, 512], BF16) for i in range(2)]
        bfb = [sb(f"bfb{i}", [128, T], BF16) for i in range(4)]
        bstage = sb("bstage", [128, 640], F32)
        maskb = sb("maskb_s", [128, 640], F32)
        xt = [sb(f"xt{i}", [128, D], F32) for i in range(2)]
        gbc = sb("gbc", [128, D], F32)
        cbf = sb("cbf_s", [128, C_TOT], BF16)
        lsel = sb("lsel_s", [48, 2048], BF16)
        srow = [sb(f"srow{i}", [48, 512], BF16) for i in range(2)]
        gains = sb("gains_s", [128, 2 * depth], F32)
        gq_s = sb("gq_s", [128, depth], F32)
        stat = [sb(f"stat{i}", [128, 4], F32) for i in range(2)]

        pacc = [ps(f"pacc{i}", [128, 512], F32) for i in range(3)]
        pB = [ps(f"pB{i}", [128, 512], F32) for i in range(2)]
        pT = [ps(f"ptr{i}", [128, 512], F32) for i in range(2)]
        ptrs = [pT[i][:, :].bitcast(BF16) for i in range(2)]
        psm = ps("psm", [128, 512], F32)

        t_hT = [[Tile(f"hT{r}_{q}") for q in range(2)] for r in range(2)]
        t_mix = [[Tile(f"mix{g}_{i}") for i in range(NT)] for g in range(2)]
        t_sel = [Tile(f"sel{h}") for h in range(2 * GMAX)]
        t_stg = Tile("stg")
        t_hsel = Tile("hsel")
        t_hxin = [Tile(f"hxin{i}") for i in range(2)]
        t_hxout = [Tile(f"hxout{i}") for i in range(2)]
        t_mxin = [Tile(f"mxin{i}") for i in range(2)]
        t_mxout = [Tile(f"mxout{i}") for i in range(2)]
        t_wh = [Tile(f"wh{i}") for i in range(3)]
        t_q = [Tile(f"q{i}") for i in range(4)]
        t_k = [Tile(f"k{i}") for i in range(4)]
        t_sg = [Tile(f"sg{i}") for i in range(4)]
        t_V = [Tile(f"V{i}") for i in range(4)]
        t_f512 = [Tile(f"f512_{i}") for i in range(4)]
        t_b512 = [Tile(f"b512_{i}") for i in range(2)]
        t_bfb = [[Tile(f"bfb{i}_{j}") for j in range(4)] for i in range(4)]
        t_bstage = Tile("bstage")
        t_maskb = Tile("maskb")
        t_xt = [[Tile(f"xt{i}_{q}") for q in range(4)] for i in range(2)]
        NXO = 8
        xo = [xt[k // 4][:, (k % 4) * 512:(k % 4 + 1) * 512] for k in range(NXO)]
        t_xo = [t_xt[k // 4][k % 4] for k in range(NXO)]
        t_gbc = Tile("gbc")
        t_c = Tile("consts")
        t_srow = [Tile(f"srow{i}") for i in range(2)]
        t_gains = Tile("gains")
        t_gq = Tile("gq")
        t_stat = [Tile(f"stat{i}") for i in range(2)]
        t_pacc = [Tile(f"pacc{i}", True) for i in range(3)]
        t_pB = [Tile(f"pB{i}", True) for i in range(2)]
        t_ptr = [Tile(f"ptr{i}", True) for i in range(2)]
        t_psm = Tile("psm", True)
        t_y = [[Tile(f"y{tt}_{dg}") for dg in range(4)] for tt in range(NTO)]

        P.op("sp", lambda e: e.dma_start(out=cbf[:, :], in_=cbf_d[:, :]), writes=[t_c], dsem="c0")
        P.op("sp", lambda e: e.dma_start(out=lsel[:, :], in_=lsel_d[:, :]), writes=[], dsem="c0")
        t_c.w = P.ops["sp"][-1]
        P.op("sp", lambda e: e.dma_start(out=maskb[:, :], in_=maskb_d[:, :]), writes=[t_maskb], dsem="c1")
        P.op("sp", lambda e: e.dma_start(out=gains[:, :], in_=gains_d[:, :]), writes=[t_gains], dsem="c2")
        P.op("sp", lambda e: e.dma_start(out=hsel[:, :], in_=hsel_d[:, :]), writes=[t_hsel], dsem="c3")
        P.op("dve", lambda e: e.tensor_scalar(gq_s[:, :], gains[:, 0:depth], SC, None, ALU.mult),
             reads=[t_gains], writes=[t_gq])

        ident = cbf[:, C_IDENT:C_IDENT + 128]
        tri = cbf[:, C_TRI:C_TRI + 128]
        ones = cbf[:, C_ONES:C_ONES + 128]
        odiv = cbf[:, C_ODIV:C_ODIV + 128]
        nega = cbf[:, C_NEGA:C_NEGA + 128]

        cc_n = [0]
        rr = {"zbank": 0, "f512a": 0, "f512b": 0, "mixslot": 0, "arg": 0, "wh": 0, "pacc": 0, "xo": 0, "xt": 0, "f512": 0, "b512": 0, "pB": 0, "ptr": 0}

        def next_rr(name, n):
            v = rr[name]
            rr[name] = (v + 1) % n
            return v

        def load_w(src_d, row0, nchunk):
            s = next_rr("wh", 3)
            dst = wh[s]
            ncol = nchunk * 256
            pieces = list(range(0, ncol, 2048))
            for pi, c0 in enumerate(pieces):
                c1 = min(ncol, c0 + 2048)
                P.op("pool", (lambda e, dst=dst, c0=c0, c1=c1: e.dma_start(
                    out=dst[:, c0 // 256:c1 // 256, :],
                    in_=src_d[row0:row0 + 128, c0:c1].rearrange("p (c j) -> p c j", j=256))),
                    writes=[t_wh[s]] if pi == 0 else [], dsem=f"wh{s}")
            t_wh[s].w = P.ops["pool"][-1]
            return s

        wseq = []
        for L_ in range(depth):
            order_ = [(g_, hs_) for g_, heads_ in enumerate(groups) for hs_ in heads_]
            for idx_, (g_, hs_) in enumerate(order_):
                row_ = (L_ * NH + hs_) * 2
                wseq.append((win_d, row_ * 128, 16))
                wseq.append((win_d, (row_ + 1) * 128, 16))
                if idx_ == len(groups[0]) and NG > 1:
                    for dg_ in range(4):
                        wseq.append((wout_d, ((L_ * NG + 0) * 4 + dg_) * 128, 2 * len(groups[0]) * 2))
            for dg_ in range(4):
                wseq.append((wout_d, ((L_ * NG + (NG - 1)) * 4 + dg_) * 128, 2 * len(groups[NG - 1]) * 2))
        wstate = {"issued": 0, "used": 0, "slots": {}}

        def w_ensure(k):
            k = min(k, len(wseq) - 1)
            while wstate["issued"] <= k:
                i = wstate["issued"]
                src_d, row0, nchunk = wseq[i]
                wstate["slots"][i] = load_w(src_d, row0, nchunk)
                wstate["issued"] += 1

        def w_use(expect):
            i = wstate["used"]
            assert wseq[i] == expect, (i, wseq[i], expect)
            w_ensure(i)
            wstate["used"] += 1
            return wstate["slots"][i]

        def mm(out, lhsT, rhs, start, stop, reads=(), writes=()):
            return P.op("pe", lambda e: e.matmul(out, lhsT, rhs, start=start, stop=stop), reads=reads, writes=writes)

        def tr(out, in_, reads=(), writes=()):
            return P.op("pe", lambda e: e.transpose(out, in_, ident), reads=reads, writes=writes)

        def act(out, in_, func, reads, writes, **kw):
            return P.op("act", lambda e: e.activation(out=out, in_=in_, func=func, **kw), reads=reads, writes=writes)

        def norm_phase(L):
            P.op("sp", lambda e: e.dma_start(out=gbc[:, :], in_=ng_d[L, :].partition_broadcast(128)),
                 writes=[t_gbc], dsem="gbc")
            for tt in range(NTO):
                s = next_rr("xt", 2)
                src = x_d if L == 0 else y_d
                rd = [] if L == 0 else list(t_y[tt])
                P.op("sp", lambda e, s=s, tt=tt, src=src: e.dma_start(out=xt[s][:, :], in_=src[tt * 128:(tt + 1) * 128, :]),
                     reads=rd, writes=t_xt[s], dsem=f"xt{s}")
                st = stat[s]
                act(bfb[0][:, :], xt[s][:, :], AF.Square, t_xt[s], t_bfb[0] + [t_stat[s]], accum_out=st[:, 0:1])
                act(st[:, 1:2], st[:, 0:1], AF.Ln, [t_stat[s]], [t_stat[s]], bias=float(EPS), scale=1.0 / D)
                act(st[:, 2:3], st[:, 1:2], AF.Exp, [t_stat[s]], [t_stat[s]], scale=-0.5)
                hb_i = 2 + (tt % 2)
                P.op("dve", lambda e, s=s, st=st, hb_i=hb_i: e.scalar_tensor_tensor(
                    out=bfb[hb_i][:, :], in0=xt[s][:, :], scalar=st[:, 2:3], in1=gbc[:, :],
                    op0=ALU.mult, op1=ALU.mult),
                    reads=t_xt[s] + [t_stat[s], t_gbc], writes=t_bfb[hb_i])
                for rnd in range(4):
                    h = next_rr("ptr", 2)
                    for c in range(4):
                        cc = rnd * 4 + c
                        tr(ptrs[h][:, c * 128:(c + 1) * 128], bfb[hb_i][:, cc * 128:(cc + 1) * 128],
                           reads=t_bfb[hb_i] + [t_c], writes=[t_ptr[h]])
                    src_v = ptrs[h][:, 0:512].rearrange("p (c j) -> p c j", j=128)
                    dst_v = hT[:, rnd * 4:(rnd + 1) * 4, tt * 128:(tt + 1) * 128]
                    if (tt // 4) % 2 == 0:
                        P.op("dve", lambda e, dst_v=dst_v, src_v=src_v: e.tensor_copy(out=dst_v, in_=src_v),
                             reads=[t_ptr[h]], writes=[t_hT[0][tt // 4]])
                    else:
                        act(dst_v, src_v, AF.Copy, [t_ptr[h]], [t_hT[0][tt // 4]])
                if tt % 4 == 3:
                    tq = tt // 4
                    P.op("sp", lambda e, tq=tq: e.dma_start(
                        out=hx_in[tq].ap().rearrange("(c p) t -> p c t", p=128),
                        in_=hT[:, :, tq * 512:(tq + 1) * 512]),
                        reads=[t_hT[0][tq]], writes=[t_hxin[tq]], dsem=f"hxo{tq}")
                    cname = f"cc{cc_n[0]}"
                    cc_n[0] += 1
                    P.op("pool", lambda e, tq=tq: e.collective_compute(
                        "AllGather", ALU.bypass, replica_groups=RG,
                        ins=[hx_in[tq].ap().opt()], outs=[hx_out[tq].ap().opt()]),
                        reads=[t_hxin[tq]], writes=[t_hxout[tq]], dsem=cname, cc=True)
            for tq in range(2):
                for r in range(2):
                    P.op("sp", lambda e, tq=tq, r=r: e.dma_start(
                        out=hT[:, :, r * TO + tq * 512: r * TO + (tq + 1) * 512],
                        in_=hx_out[tq].ap()[r * D:(r + 1) * D, :].rearrange("(c p) t -> p c t", p=128)),
                        reads=[t_hxout[tq]], writes=[t_hT[r][tq]], dsem=f"hxl{r}{tq}")

        def in_proj(L, hs, is_b, tg_major=False):
            pend = []
            vpend = []

            def flush_pend():
                while pend:
                    pend.pop(0)()
            row = (L * NH + hs) * 2
            s_qk = w_use((win_d, row * 128, 16))
            s_vg = w_use((win_d, (row + 1) * 128, 16))
            if tg_major:
                order_ct = [(comp, tg) for tg in (0, 2, 1, 3) for comp in range(4)]
            else:
                order_ct = [(comp, tg) for comp in range(4) for tg in range(4)]
            for comp, tg in order_ct:
                if True:
                    s = s_qk if comp < 2 else s_vg
                    co = (comp % 2) * 128
                    a = next_rr("pacc", 3)
                    for c in range(DC):
                        mm(pacc[a][:, :], wh[s][:, c, co:co + 128], hT[:, c, tg * 512:(tg + 1) * 512],
                           c == 0, c == DC - 1,
                           reads=[t_wh[s], t_hT[tg // 2][tg % 2]],
                           writes=[t_pacc[a]])
                    cs = slice(tg * 512, (tg + 1) * 512)
                    if comp >= 2:
                        flush_pend()
                    while vpend:
                        vpend.pop(0)()
                    if comp == 2:
                        vT = bfb[1]
                        P.op("dve", lambda e, a=a, cs=cs: e.tensor_copy(out=bfb[1][:, cs], in_=pacc[a][:, :]),
                             reads=[t_pacc[a]], writes=[t_bfb[1][tg]])
                        def vtr(tg=tg):
                            h = next_rr("ptr", 2)
                            for b in range(4):
                                tb = tg * 4 + b
                                tr(ptrs[h][:, b * 128:(b + 1) * 128], bfb[1][:, tb * 128:(tb + 1) * 128],
                                   reads=[t_bfb[1][tg], t_c], writes=[t_ptr[h]])
                            act(Vt[:, tg * 4:(tg + 1) * 4, :],
                                ptrs[h][:, 0:512].rearrange("p (c j) -> p c j", j=128),
                                AF.Copy, [t_ptr[h]], [t_V[tg]])
                        vpend.append(vtr)
                    elif comp == 3:
                        f = next_rr("f512", 4)
                        act(f512[f][:, :], pacc[a][:, :], AF.Exp, [t_pacc[a]], [t_f512[f]], scale=-1.0)
                        act(f512[f][:, :], f512[f][:, :], AF.Ln, [t_f512[f]], [t_f512[f]], bias=1.0)
                        act(f512[f][:, :], f512[f][:, :], AF.Exp, [t_f512[f]], [t_f512[f]], scale=-1.0)
                        P.op("dve", lambda e, a=a, f=f, cs=cs: e.tensor_tensor(
                            out=sgT[:, cs], in0=pacc[a][:, :], in1=f512[f][:, :], op=ALU.mult),
                            reads=[t_pacc[a], t_f512[f]], writes=[t_sg[tg]])
                    elif not is_b:
                        if comp == 0:
                            P.op("dve", lambda e, a=a, cs=cs: e.tensor_scalar(qT[:, cs], pacc[a][:, :], SC, None, ALU.mult),
                                 reads=[t_pacc[a]], writes=[t_q[tg]])
                        else:
                            act(kT[:, cs], pacc[a][:, :], AF.Copy, [t_pacc[a]], [t_k[tg]])
                    else:
                        fr = next_rr("f512a", 2)
                        P.op("dve", lambda e, a=a, fr=fr: e.tensor_copy(out=f512[fr][:, :], in_=pacc[a][:, :]),
                             reads=[t_pacc[a]], writes=[t_f512[fr]])
                        bq = next_rr("b512", 2)
                        act(b512[bq][:, :], f512[fr][:, :], AF.Square, [t_f512[fr]], [t_b512[bq]])
                        def post(fr=fr, bq=bq, comp=comp, cs=cs, tg=tg):
                            u = next_rr("pB", 2)
                            mm(pB[u][:, :], odiv, b512[bq][:, :], True, True, reads=[t_b512[bq], t_c], writes=[t_pB[u]])
                            f2 = 2 + next_rr("f512b", 2)
                            act(f512[f2][:, :], pB[u][:, :], AF.Ln, [t_pB[u]], [t_f512[f2]], bias=float(EPS))
                            act(f512[f2][:, :], f512[f2][:, :], AF.Exp, [t_f512[f2]], [t_f512[f2]], scale=-0.5)
                            dst = qT if comp == 0 else kT
                            tdst = t_q if comp == 0 else t_k
                            gsc = gq_s[:, L:L + 1] if comp == 0 else gains[:, depth + L:depth + L + 1]
                            P.op("dve", lambda e: e.scalar_tensor_tensor(
                                out=dst[:, cs], in0=f512[fr][:, :], scalar=gsc, in1=f512[f2][:, :],
                                op0=ALU.mult, op1=ALU.mult),
                                reads=[t_f512[fr], t_f512[f2], t_gq, t_gains], writes=[tdst[tg]])
                        flush_pend()
                        pend.append(post)
            while vpend:
                vpend.pop(0)()

        def attn_a(ms):
            def blk(i):
                return slice(i * 128, (i + 1) * 128)

            def q512(I):
                return slice(I * 512, (I + 1) * 512)

            def sp_ap(j):
                return bfb[j // 4][:, (j % 4) * 512:(j % 4 + 1) * 512]

            def sp_tile(j):
                return t_bfb[j // 4][j % 4]

            def c0_of(I, j):
                return max(0, j - 4 * I) * 128

            def zstep(I, j):
                diag = j >= 4 * I
                c0 = c0_of(I, j)
                zi = next_rr("zbank", 4)
                zb = (pB[0], pB[1], pT[0], pT[1])[zi]
                t_zb = (t_pB[0], t_pB[1], t_ptr[0], t_ptr[1])[zi]
                mm(zb[:, c0:512], kT[:, blk(j)], qT[:, I * 512 + c0:(I + 1) * 512], True, not diag,
                   reads=[t_k[j // 4], t_q[I]], writes=[t_zb])
                if diag:
                    jd = j - 4 * I
                    mm(zb[:, c0:512], ident, cbf[:, C_NEGW + jd * 512 + c0:C_NEGW + (jd + 1) * 512], False, True,
                       reads=[t_c], writes=[t_zb])
                f = next_rr("f512", 4)
                act(f512[f][:, c0:512], zb[:, c0:512], AF.Exp, [t_zb], [t_f512[f]])

                def z2():
                    act(sp_ap(j)[:, c0:512], f512[f][:, c0:512], AF.Ln, [t_f512[f]], [sp_tile(j)], bias=1.0)
                return z2

            def srow_step(I):
                u = I % 2
                nkb = 4 * I + 4
                for j in range(nkb):
                    c0 = c0_of(I, j)
                    mm(pacc[2][0:48, c0:512], cbf[:, C_EP + j * 48:C_EP + (j + 1) * 48], sp_ap(j)[:, c0:512],
                       j == 0, j == nkb - 1, reads=[sp_tile(j), t_c], writes=[t_pacc[2]])
                P.op("dve", lambda e: e.tensor_copy(out=srow[u][0:48, :], in_=pacc[2][0:48, :]),
                     reads=[t_pacc[2]], writes=[t_srow[u]])
                P.op("dve", lambda e: e.tensor_tensor(out=srow[u][32:48, :], in0=pacc[2][32:48, :],
                                                      in1=srow[u][32:48, :], op=ALU.subtract),
                     reads=[t_pacc[2], t_srow[u]], writes=[t_srow[u]])

            pend = []
            pend_w = []

            def astep(I, j):
                u = I % 2
                nkb = 4 * I + 4
                diag = j >= 4 * I
                last = (j == nkb - 1)
                a = next_rr("arg", 2)
                c0 = c0_of(I, j)
                o = pacc[a][:, c0:512]
                mm(o, kT[:, blk(j)], qT[:, I * 512 + c0:(I + 1) * 512], True, False,
                   reads=[t_k[j // 4], t_q[I]], writes=[t_pacc[a]])
                mm(o, tri, sp_ap(j)[:, c0:512], False, False, reads=[sp_tile(j), t_c], writes=[t_pacc[a]])
                if not last:
                    mm(o, lsel[0:48, blk(j)], srow[u][0:48, c0:512], False, not diag,
                       reads=[t_srow[u], t_c], writes=[t_pacc[a]])
                if diag:
                    jd = j - 4 * I
                    mm(o, ident, cbf[:, C_NEGW + jd * 512 + c0:C_NEGW + (jd + 1) * 512], False, True,
                       reads=[t_c], writes=[t_pacc[a]])
                bq = next_rr("b512", 2)

                def expw():
                    act(b512[bq][:, c0:512], pacc[a][:, c0:512], AF.Exp, [t_pacc[a]], [t_b512[bq]])

                def wv():
                    mm(psm[:, c0:512], Vt[:, j, :], b512[bq][:, c0:512], j == 0, j == nkb - 1,
                       reads=[t_V[j // 4], t_b512[bq]], writes=[t_psm])
                while pend:
                    pend.pop(0)()
                while pend_w:
                    ew, v = pend_w.pop(0)
                    ew()
                    pend.append(v)
                pend_w.append((expw, wv))

            NI = NT // 4
            zq = []
            for j in range(4):
                zq.append(zstep(0, j))
                if len(zq) > 1:
                    zq.pop(0)()
            while zq:
                zq.pop(0)()
            srow_step(0)
            for I in range(NI):
                nkb = 4 * I + 4
                for j in range(nkb):
                    z2 = zstep(I + 1, j) if I + 1 < NI else None
                    astep(I, j)
                    if z2 is not None:
                        z2()
                while pend:
                    pend.pop(0)()
                while pend_w:
                    ew, v = pend_w.pop(0)
                    ew()
                    v()
                P.op("dve", lambda e, I=I: e.tensor_tensor(out=mixT[:, ms, q512(I)], in0=psm[:, :], in1=sgT[:, q512(I)],
                                                          op=ALU.mult),
                     reads=[t_psm, t_sg[I]], writes=t_mix[ms][4 * I:4 * I + 4])
                if I + 1 < NI:
                    for j in range(nkb, nkb + 4):
                        zq.append(zstep(I + 1, j))
                        if len(zq) > 1:
                            zq.pop(0)()
                    while zq:
                        zq.pop(0)()
                    srow_step(I + 1)

        bm_done = set()

        def prep_bm(L, hb):
            bmi = 2 + hb % 2
            bm, t_bm = bfb[bmi], t_bfb[bmi][0:2]
            r0 = (L * n_hb + hb) * 128
            P.op("sp", lambda e: e.dma_start(out=bstage[:, :], in_=bias_d[r0:r0 + 128, :]), writes=[t_bstage], dsem="bst")
            P.op("dve", lambda e: e.tensor_tensor(out=bm[:, 0:640], in0=bstage[:, :], in1=maskb[:, :], op=ALU.add),
                 reads=[t_bstage, t_maskb], writes=t_bm)
            bm_done.add((L, hb))

        def attn_b(L, ms, hb):
            def blk(i):
                return slice(i * 128, (i + 1) * 128)

            bmi = 2 + hb % 2
            bm, t_bm = bfb[bmi], t_bfb[bmi][0:2]
            if (L, hb) not in bm_done:
                prep_bm(L, hb)

            def pslot(m):
                k = m % 3
                if k < 2:
                    return bfb[0], k * 1024, t_bfb[0][k * 2:k * 2 + 2]
                return bfb[1], 0, t_bfb[1][0:2]

            def sset(m):
                k = m % 3
                if k < 2:
                    return pB[k], t_pB[k], pacc[k], t_pacc[k]
                return pT[0], t_ptr[0], pT[1], t_ptr[1]

            def scores(m):
                jmin = max(0, m - 4)
                js = list(range(jmin, m + 1))
                bk, t_bk, ov, t_ov = sset(m)
                for jj, j in enumerate(js):
                    if jj < 4:
                        o, tw = bk[:, blk(jj)], t_bk
                    else:
                        o, tw = ov[:, 0:128], t_ov
                    mm(o, kT[:, blk(j)], qT[:, blk(m)], True, False, reads=[t_k[j // 4], t_q[m // 4]], writes=[tw])
                    mm(o, ident, bm[:, blk(m - j)], False, True, reads=t_bm + [t_c], writes=[tw])
                pb, base, tp = pslot(m)
                n4 = min(len(js), 4) * 128
                act(pb[:, base:base + n4], bk[:, 0:n4], AF.Exp, [t_bk], tp)
                if len(js) == 5:
                    act(pb[:, base + 512:base + 640], ov[:, 0:128], AF.Exp, [t_ov], tp)

            def dennum(m):
                jmin = max(0, m - 4)
                js = list(range(jmin, m + 1))
                pb, base, tp = pslot(m)
                rd = m % 2
                dn, t_dn = ((pacc[2], t_pacc[2]), (psm, t_psm))[(m // 2) % 2]
                for jj, j in enumerate(js):
                    mm(dn[:, blk(rd)], ones, pb[:, base + jj * 128:base + (jj + 1) * 128],
                       jj == 0, jj == len(js) - 1, reads=tp + [t_c], writes=[t_dn])
                for jj, j in enumerate(js):
                    mm(dn[:, blk(2 + rd)], Vt[:, j, :], pb[:, base + jj * 128:base + (jj + 1) * 128],
                       jj == 0, jj == len(js) - 1, reads=tp + [t_V[j // 4]], writes=[t_dn])

            def epi(mp):
                cs = slice(mp * 256, (mp + 1) * 256)
                f1 = next_rr("f512", 4)
                dn, t_dn = ((pacc[2], t_pacc[2]), (psm, t_psm))[mp % 2]
                act(f512[f1][:, 0:256], dn[:, 0:256], AF.Ln, [t_dn], [t_f512[f1]])
                act(f512[f1][:, 0:256], f512[f1][:, 0:256], AF.Exp, [t_f512[f1]], [t_f512[f1]], scale=-1.0)
                f2 = next_rr("f512", 4)
                P.op("dve", lambda e: e.tensor_tensor(out=f512[f2][:, 0:256], in0=dn[:, 256:512], in1=f512[f1][:, 0:256],
                                                      op=ALU.mult),
                     reads=[t_dn, t_f512[f1]], writes=[t_f512[f2]])
                P.op("dve", lambda e: e.tensor_tensor(out=mixT[:, ms, cs], in0=f512[f2][:, 0:256], in1=sgT[:, cs],
                                                      op=ALU.mult),
                     reads=[t_f512[f2], t_sg[mp // 2]], writes=[t_mix[ms][2 * mp], t_mix[ms][2 * mp + 1]])

            for n in range(NT + 2):
                if n < NT:
                    scores(n)
                if 0 <= n - 2 < NT:
                    dennum(n - 2)
                    if (n - 2) % 2 == 1:
                        epi((n - 2) // 2)

        def out_proj(L, g, first):
            nh = 2 * len(groups[g])
            steps = [(dg, tt) for dg in range(4) for tt in range(NTO)]
            slots = {}

            def load(k):
                dg, tt = steps[k]
                so = next_rr("xo", NXO)
                slots[k] = so
                src = x_d if first else y_d
                P.op("sp", lambda e: e.dma_start(out=xo[so], in_=src[tt * 128:(tt + 1) * 128, dg * 512:(dg + 1) * 512]),
                     reads=[] if first else [t_y[tt][dg]], writes=[t_xo[so]], dsem=f"xo{so}")

            PF = 5
            for k0 in range(PF):
                load(k0)
            s = None
            for k, (dg, tt) in enumerate(steps):
                if tt == 0:
                    s = w_use((wout_d, ((L * NG + g) * 4 + dg) * 128, nh * 2))
                    if dg >= 1:
                        w_ensure(wstate["used"])
                if k + PF < len(steps):
                    load(k + PF)
                a = next_rr("pacc", 3)
                for hh in range(nh):
                    mm(pacc[a][:, :], mixSel[:, hh, tt * 128:(tt + 1) * 128],
                       wh[s][:, 2 * hh:2 * hh + 2, :].rearrange("p a b -> p (a b)"),
                       hh == 0, hh == nh - 1, reads=[t_sel[hh], t_wh[s]], writes=[t_pacc[a]])
                so = slots[k]
                P.op("dve", lambda e, so=so, a=a: e.tensor_tensor(out=xo[so], in0=xo[so], in1=pacc[a][:, :],
                                                                  op=ALU.add),
                     reads=[t_xo[so], t_pacc[a]], writes=[t_xo[so]])
                P.op("sp", lambda e, so=so, tt=tt, dg=dg: e.dma_start(
                    out=y_d[tt * 128:(tt + 1) * 128, dg * 512:(dg + 1) * 512], in_=xo[so]),
                    reads=[t_xo[so]], writes=[t_y[tt][dg]], dsem=f"ys{so}")

        def exchange(ms, hh, nhg):
            P.op("sp", lambda e: e.dma_start(out=mx_in[ms].ap(), in_=mixT[:, ms, :]),
                 reads=t_mix[ms], writes=[t_mxin[ms]], dsem=f"mxo{ms}")
            cname = f"cc{cc_n[0]}"
            cc_n[0] += 1
            P.op("pool", lambda e: e.collective_compute(
                "AllGather", ALU.bypass, replica_groups=RG,
                ins=[mx_in[ms].ap().opt()], outs=[mx_out[ms].ap().opt()]),
                reads=[t_mxin[ms]], writes=[t_mxout[ms]], dsem=cname, cc=True)

        def exchange_recv(ms, hh, nhg):
            P.op("sp", lambda e: e.dma_start(out=stg_lo[:, :, :],
                                             in_=mx_out[ms].ap()[:, 0:TO].rearrange("(r p) t -> p r t", p=128)),
                 reads=[t_mxout[ms]], writes=[t_stg], dsem="stg")
            P.op("sp", lambda e: e.dma_start(out=stg_hi[:, :, :],
                                             in_=mx_out[ms].ap()[:, TO:T].rearrange("(r p) t -> p r t", p=128)),
                 reads=[t_mxout[ms]], writes=[], dsem="stg")
            t_stg.w = P.ops["sp"][-1]
            P.op("dve", lambda e: e.tensor_scalar(stg_hi[:, :, :], stg_hi[:, :, :], hsel[:, 1:2], None, ALU.mult),
                 reads=[t_stg, t_hsel], writes=[t_stg])
            for r in range(2):
                hidx = r * nhg + hh
                P.op("dve", lambda e, r=r, hidx=hidx: e.scalar_tensor_tensor(
                    out=mixSel[:, hidx, :], in0=stg_lo[:, r, :], scalar=hsel[:, 0:1], in1=stg_hi[:, r, :],
                    op0=ALU.mult, op1=ALU.add),
                    reads=[t_stg, t_hsel], writes=[t_sel[hidx]])

        for L in range(depth):
            if dbg >= 1:
                P.phase = "norm"
                norm_phase(L)
            pending_op = None
            pending_rx = []

            def flush_rx():
                P.phase = "exch"
                while pending_rx:
                    exchange_recv(*pending_rx.pop(0))

            for g, heads in enumerate(groups):
                for hh, hs in enumerate(heads):
                    is_b = hs >= n_ha
                    ms = next_rr("mixslot", 2)
                    if dbg >= 2:
                        P.phase = "inproj_b" if is_b else "inproj_a"
                        if is_b and hs - n_ha >= 1:
                            prep_bm(L, hs - n_ha)
                        in_proj(L, hs, is_b, tg_major=(g == 0 and hh == 0 and not is_b))
                        w_ensure(wstate["used"] + 1)
                    flush_rx()
                    if pending_op is not None:
                        P.phase = "outproj"
                        out_proj(L, pending_op, first=(L == 0 and pending_op == 0))
                        pending_op = None
                    if dbg >= 3:
                        P.phase = "attn_b" if is_b else "attn_a"
                        if is_b:
                            attn_b(L, ms, hs - n_ha)
                        else:
                            attn_a(ms)
                        P.phase = "exch"
                        exchange(ms, hh, len(heads))
                        pending_rx.append((ms, hh, len(heads)))
                if dbg >= 4:
                    if g + 1 < len(groups):
                        pending_op = g
                    else:
                        flush_rx()
                        P.phase = "outproj"
                        out_proj(L, g, first=(L == 0 and g == 0))

        dnames = sorted(P.dma_cnt.keys())
        sem_objs = {}
        for e in Prog.ENGS:
            sem_objs[e] = es.enter_context(nc.semaphore(f"s_{e}"))
        dsem_objs = {}
        for dn in dnames:
            dsem_objs[dn] = es.enter_context(nc.semaphore(f"d_{dn}"))
        block = es.enter_context(nc.Block())
        P.emit(block, sem_objs, dsem_objs)
    nc._prog_stats = {e: (len(P.ops[e]), P.sig_counts[e], P.nwaits[e]) for e in Prog.ENGS}
    nc._pe_phases = [o.ph for o in P.ops["pe"]]
    return nc


def prep_inputs(x, norm_g, w_in, q_norm_g, k_norm_g, rel_bias, w_out, depth=DEPTH, n_ha=NHA_C, n_hb=NHB_C,
                n_cores=N_CORES):
    NH = n_ha + n_hb
    cbf, lsel, maskb = host_consts()
    w_in = np.asarray(w_in, np.float32)
    w_out = np.asarray(w_out, np.float32)
    xs = np.asarray(x, np.float32)
    rb_all = np.asarray(rel_bias, np.float32)
    groups = []
    if n_ha:
        groups.append(("a", n_ha))
    if n_hb:
        groups.append(("b", n_hb))
    GMAX = max(n for _, n in groups)
    wout = np.zeros((depth, len(groups), 4, 128, 2 * GMAX, 512), np.float32)
    for gi, (kind, n) in enumerate(groups):
        for hidx in range(2 * n):
            e0 = hidx * 128 if kind == "a" else 1024 + hidx * 128
            for dg in range(4):
                wout[:, gi, dg, :, hidx, :] = w_out[:depth, e0:e0 + 128, dg * 512:(dg + 1) * 512]
    wout = wout.reshape(depth * len(groups) * 4 * 128, 2 * GMAX * 512)
    gains = np.ascontiguousarray(np.concatenate(
        [np.asarray(q_norm_g, np.float32)[:depth].T, np.asarray(k_norm_g, np.float32)[:depth].T], axis=1))
    s_i = np.arange(128)[:, None, None]
    dl = np.arange(5)[None, :, None]
    t_i = np.arange(128)[None, None, :]
    idx = np.clip(dl * 128 + t_i - s_i, -63, 256) + 63
    shared = {"wout": wout, "ng": np.ascontiguousarray(np.asarray(norm_g, np.float32)[:depth]),
              "gains": gains, "maskb": maskb, "cbf": cbf, "lsel": lsel}
    per_half = []
    for g in range(2):
        win = np.empty((depth, NH, 2, 128, 16, 256), np.float32)
        for hs in range(NH):
            if hs < n_ha:
                base, h = 0, g * n_ha + hs
            else:
                base, h = 4096, g * n_hb + (hs - n_ha)
            for comp in range(4):
                col0 = base + comp * 1024 + h * 128
                blk = w_in[:depth, :, col0:col0 + 128].reshape(depth, 16, 128, 128)
                win[:, hs, comp // 2, :, :, (comp % 2) * 128:(comp % 2) * 128 + 128] = blk.transpose(0, 2, 1, 3)
        win = win.reshape(depth * NH * 2 * 128, 16 * 256)
        nb = max(n_hb, 1)
        rb = rb_all[:depth, g * n_hb:g * n_hb + nb]
        biasT = np.ascontiguousarray(rb[:, :, idx]).reshape(depth * nb * 128, 640)
        hsel = np.zeros((128, 2), np.float32)
        hsel[:, g] = 1.0
        per_half.append({"win": win, "biasT": biasT, "hsel": hsel})
    in_maps = []
    for c in range(n_cores):
        b, g = c // 2, c % 2
        m = dict(shared)
        m.update(per_half[g])
        m["x"] = np.ascontiguousarray(xs[b % xs.shape[0], g * TO:(g + 1) * TO])
        in_maps.append(m)
    return in_maps


_NC_CACHE = {}


def kernel(x, norm_g, w_in, q_norm_g, k_norm_g, rel_bias, w_out):
    if "full" not in _NC_CACHE:
        _NC_CACHE["full"] = build_program()
    nc = _NC_CACHE["full"]
    in_maps = prep_inputs(x, norm_g, w_in, q_norm_g, k_norm_g, rel_bias, w_out)
    res = run_bass_kernel_spmd(nc, in_maps, core_ids=list(range(N_CORES)))
    B = np.asarray(x).shape[0]
    out = np.empty((B, T, D), np.float32)
    for c in range(N_CORES):
        b, g = c // 2, c % 2
        out[b, g * TO:(g + 1) * TO] = np.asarray(res.results[c]["y"], np.float32)
    return out
```

```python
import contextlib
import numpy as np
import ml_dtypes
import concourse.bass as bass
import concourse.mybir as mybir
from concourse.bass_utils import run_bass_kernel_spmd

F32 = mybir.dt.float32
BF16 = mybir.dt.bfloat16
AF = mybir.ActivationFunctionType
ALU = mybir.AluOpType

D = 2048
T = 2048
NT = T // 128
DC = D // 128
DEPTH = 4
NHA = 8
NHB = 8
HD = 128
EPS = 1e-6
NEG = -30000.0
SAME_ENG_SYNC = True
N_CORES = 8
TO = T // 2
NTO = TO // 128
NHA_C = NHA // 2
NHB_C = NHB // 2


class Tile:
    __slots__ = ("name", "w", "r", "excl")

    def __init__(self, name, excl=False):
        self.name = name
        self.w = None
        self.r = {}
        self.excl = excl


class Op:
    __slots__ = ("eng", "fn", "deps", "needed", "sig", "dsem", "dval", "is_dma", "cc", "ph")

    def __init__(self, eng, fn):
        self.eng = eng
        self.fn = fn
        self.deps = []
        self.needed = False
        self.sig = 0
        self.dsem = None
        self.dval = 0
        self.is_dma = False
        self.cc = False
        self.ph = ""


class Prog:
    ENGS = ("pe", "act", "dve", "pool", "sp")

    def __init__(self):
        self.ops = {e: [] for e in self.ENGS}
        self.dma_cnt = {}
        self.phase = ""

    def op(self, eng, fn, reads=(), writes=(), dsem=None, cc=False):
        o = Op(eng, fn)
        o.ph = self.phase
        deps = {}
        for t in reads:
            if t.w is not None:
                deps[id(t.w)] = t.w
            if t.excl:
                for k, v in t.r.items():
                    if k != eng and not isinstance(v, list):
                        deps[id(v)] = v
        for t in writes:
            if t.w is not None:
                deps[id(t.w)] = t.w
            for k, v in t.r.items():
                if isinstance(v, list):
                    for d in v:
                        deps[id(d)] = d
                else:
                    deps[id(v)] = v
        o.deps = list(deps.values())
        if dsem is not None:
            o.is_dma = True
            o.dsem = dsem
            inc = 1 if cc else 16
            self.dma_cnt[dsem] = self.dma_cnt.get(dsem, 0) + inc
            o.dval = self.dma_cnt[dsem]
            o.cc = cc
        for t in reads:
            if o.is_dma:
                t.r.setdefault("dma", []).append(o)
            else:
                t.r[eng] = o
        for t in writes:
            t.w = o
            t.r = {}
        self.ops[eng].append(o)
        return o

    def _skip(self, o, d):
        return (not d.is_dma) and (not o.is_dma) and d.eng == o.eng and (d.eng == "pe" or not SAME_ENG_SYNC)

    def emit(self, block, sems, dsems):
        for e in self.ENGS:
            for o in self.ops[e]:
                for d in o.deps:
                    if d.is_dma or self._skip(o, d):
                        continue
                    d.needed = True
        for e in self.ENGS:
            c = 0
            for o in self.ops[e]:
                if o.needed and not o.is_dma:
                    c += 1
                    o.sig = c
        self.sig_counts = {e: sum(1 for o in self.ops[e] if o.needed and not o.is_dma) for e in self.ENGS}
        self.nwaits = {e: 0 for e in self.ENGS}

        def run(engname):
            def body(eng):
                waited = {}
                for o in self.ops[engname]:
                    need = {}
                    for d in o.deps:
                        if d.is_dma:
                            key, sem, val = ("d", d.dsem), dsems[d.dsem], d.dval
                        else:
                            if self._skip(o, d):
                                continue
                            key, sem, val = ("e", d.eng), sems[d.eng], d.sig
                        if key not in need or need[key][1] < val:
                            need[key] = (sem, val)
                    for key, (sem, val) in need.items():
                        if waited.get(key, 0) >= val:
                            continue
                        waited[key] = val
                        eng.wait_ge(sem, val)
                        self.nwaits[engname] += 1
                    inst = o.fn(eng)
                    if o.cc:
                        inst.then_inc(dsems[o.dsem])
                    elif o.is_dma:
                        inst.then_inc(dsems[o.dsem], 16)
                    elif o.needed:
                        inst.then_inc(sems[engname], 1)
                fin = {}
                for o in self.ops[engname]:
                    if o.is_dma:
                        fin[o.dsem] = max(fin.get(o.dsem, 0), o.dval)
                for k, v in fin.items():
                    eng.wait_ge(dsems[k], v)
            return body

        block.tensor(run("pe"))
        block.scalar(run("act"))
        block.vector(run("dve"))
        block.gpsimd(run("pool"))
        block.sync(run("sp"))


C_IDENT, C_TRI, C_ONES, C_ODIV, C_NEGA = 0, 128, 256, 384, 512
C_EP = 640
C_NEGW = 640 + 16 * 48
C_TOT = C_NEGW + 2048


def host_consts():
    c = np.zeros((128, C_TOT), np.float32)
    c[:, C_IDENT:C_IDENT + 128] = np.eye(128)
    sp_, s_ = np.meshgrid(np.arange(128), np.arange(128), indexing="ij")
    c[:, C_TRI:C_TRI + 128] = np.where(sp_ >= s_, -1.0, 0.0)
    c[:, C_ONES:C_ONES + 128] = 1.0
    c[:, C_ODIV:C_ODIV + 128] = 1.0 / 128.0
    s_i, t_i = np.meshgrid(np.arange(128), np.arange(128), indexing="ij")
    c[:, C_NEGA:C_NEGA + 128] = np.where(s_i < t_i, 0.0, NEG)
    trm = np.where(s_i < t_i, 0.0, NEG)
    for jd in range(4):
        for b in range(4):
            c0 = C_NEGW + jd * 512 + b * 128
            if b < jd:
                c[:, c0:c0 + 128] = NEG
            elif b == jd:
                c[:, c0:c0 + 128] = trm
    for j in range(16):
        c[:, C_EP + j * 48 + j] = 1.0
        c[:, C_EP + j * 48 + 32 + j] = 1.0
    lsel = np.zeros((48, 16 * 128), np.float32)
    for j in range(16):
        for k in range(48):
            if (k < 16 or k >= 32) and (k % 32) > j:
                lsel[k, j * 128:(j + 1) * 128] = -1.0
    maskb = np.zeros((128, 5, 128), np.float32)
    maskb[:64, 4, 64:] = NEG
    maskb[64:, 0, :64] = NEG
    return (c.astype(ml_dtypes.bfloat16), lsel.astype(ml_dtypes.bfloat16),
            maskb.reshape(128, 640).astype(np.float32))


def build_program(depth=DEPTH, n_ha=NHA_C, n_hb=NHB_C, dbg=9, n_cores=N_CORES):
    nc = bass.Bass("TRN2", target_bir_lowering=False)
    NH = n_ha + n_hb
    groups = []
    if n_ha:
        groups.append(list(range(0, n_ha)))
    if n_hb:
        groups.append(list(range(n_ha, NH)))
    NG = len(groups)
    GMAX = max(len(g) for g in groups)
    SC = float(HD ** -0.5)

    x_d = nc.dram_tensor("x", [TO, D], F32, kind="ExternalInput").ap()
    y_d = nc.dram_tensor("y", [TO, D], F32, kind="ExternalOutput").ap()
    hsel_d = nc.dram_tensor("hsel", [128, 2], F32, kind="ExternalInput").ap()
    hx_in = [nc.dram_tensor(f"hx_in{i}", [D, 512], BF16) for i in range(2)]
    hx_out = [nc.dram_tensor(f"hx_out{i}", [2 * D, 512], BF16) for i in range(2)]
    mx_in = [nc.dram_tensor(f"mx_in{i}", [128, T], BF16) for i in range(2)]
    mx_out = [nc.dram_tensor(f"mx_out{i}", [256, T], BF16) for i in range(2)]
    RG = [[2 * i, 2 * i + 1] for i in range(n_cores // 2)]
    win_d = nc.dram_tensor("win", [depth * NH * 2 * 128, 16 * 256], F32, kind="ExternalInput").ap()
    wout_d = nc.dram_tensor("wout", [depth * NG * 4 * 128, 2 * GMAX * 512], F32, kind="ExternalInput").ap()
    ng_d = nc.dram_tensor("ng", [depth, D], F32, kind="ExternalInput").ap()
    gains_d = nc.dram_tensor("gains", [128, 2 * depth], F32, kind="ExternalInput").ap()
    bias_d = nc.dram_tensor("biasT", [depth * max(n_hb, 1) * 128, 640], F32, kind="ExternalInput").ap()
    maskb_d = nc.dram_tensor("maskb", [128, 640], F32, kind="ExternalInput").ap()
    cbf_d = nc.dram_tensor("cbf", [128, C_TOT], BF16, kind="ExternalInput").ap()
    lsel_d = nc.dram_tensor("lsel", [48, 2048], BF16, kind="ExternalInput").ap()

    P = Prog()
    es = contextlib.ExitStack()

    def sb(name, shape, dt):
        return es.enter_context(nc.sbuf_tensor(name, shape, dt))

    def ps(name, shape, dt):
        return es.enter_context(nc.psum_tensor(name, shape, dt))

    with es:
        hT = sb("hT", [128, DC, T], BF16)
        mixT = sb("mixT", [128, 2, T], BF16)
        mixSel = sb("mixSel", [128, 2 * GMAX, TO], BF16)
        stg_lo = sb("stg_lo", [128, 2, TO], BF16)
        stg_hi = sb("stg_hi", [128, 2, TO], BF16)
        hsel = sb("hsel_s", [128, 2], F32)
        wh = [sb(f"wh{i}", [128, 16, 256], BF16) for i in range(3)]
        qT = sb("qT", [128, T], BF16)
        kT = sb("kT", [128, T], BF16)
        sgT = sb("sgT", [128, T], BF16)
        Vt = sb("Vt", [128, NT, 128], BF16)
        f512 = [sb(f"f512_{i}", [128, 512], F32) for i in range(4)]
        b512 = [sb(f"b512_{i}", [128, 512], BF16) for i in range(2)]
        bfb = [sb(f"bfb{i}", [128, T], BF16) for i in range(4)]
        bstage = sb("bstage", [128, 640], F32)
        maskb = sb("maskb_s", [128, 640], F32)
        xt = [sb(f"xt{i}", [128, D], F32) for i in range(2)]
        gbc = sb("gbc", [128, D], F32)
        cbf = sb("cbf_s", [128, C_TOT], BF16)
        lsel = sb("lsel_s", [48, 2048], BF16)
        srow = [sb(f"srow{i}", [48, 512], BF16) for i in range(2)]
        gains = sb("gains_s", [128, 2 * depth], F32)
        gq_s = sb("gq_s", [128, depth], F32)
        stat = [sb(f"stat{i}", [128, 4], F32) for i in range(2)]

        pacc = [ps(f"pacc{i}", [128, 512], F32) for i in range(3)]
        pB = [ps(f"pB{i}", [128, 512], F32) for i in range(2)]
        pT = [ps(f"ptr{i}", [128, 512], F32) for i in range(2)]
        ptrs = [pT[i][:, :].bitcast(BF16) for i in range(2)]
        psm = ps("psm", [128, 512], F32)

        t_hT = [[Tile(f"hT{r}_{q}") for q in range(2)] for r in range(2)]
        t_mix = [[Tile(f"mix{g}_{i}") for i in range(NT)] for g in range(2)]
        t_sel = [Tile(f"sel{h}") for h in range(2 * GMAX)]
        t_stg = Tile("stg")
        t_hsel = Tile("hsel")
        t_hxin = [Tile(f"hxin{i}") for i in range(2)]
        t_hxout = [Tile(f"hxout{i}") for i in range(2)]
        t_mxin = [Tile(f"mxin{i}") for i in range(2)]
        t_mxout = [Tile(f"mxout{i}") for i in range(2)]
        t_wh = [Tile(f"wh{i}") for i in range(3)]
        t_q = [Tile(f"q{i}") for i in range(4)]
        t_k = [Tile(f"k{i}") for i in range(4)]
        t_sg = [Tile(f"sg{i}") for i in range(4)]
        t_V = [Tile(f"V{i}") for i in range(4)]
        t_f512 = [Tile(f"f512_{i}") for i in range(4)]
        t_b512 = [Tile(f"b512_{i}") for i in range(2)]
        t_bfb = [[Tile(f"bfb{i}_{j}") for j in range(4)] for i in range(4)]
        t_bstage = Tile("bstage")
        t_maskb = Tile("maskb")
        t_xt = [[Tile(f"xt{i}_{q}") for q in range(4)] for i in range(2)]
        NXO = 8
        xo = [xt[k // 4][:, (k % 4) * 512:(k % 4 + 1) * 512] for k in range(NXO)]
        t_xo = [t_xt[k // 4][k % 4] for k in range(NXO)]
        t_gbc = Tile("gbc")
        t_c = Tile("consts")
        t_srow = [Tile(f"srow{i}") for i in range(2)]
        t_gains = Tile("gains")
        t_gq = Tile("gq")
        t_stat = [Tile(f"stat{i}") for i in range(2)]
        t_pacc = [Tile(f"pacc{i}", True) for i in range(3)]
        t_pB = [Tile(f"pB{i}", True) for i in range(2)]
        t_ptr = [Tile(f"ptr{i}", True) for i in range(2)]
        t_psm = Tile("psm", True)
        t_y = [[Tile(f"y{tt}_{dg}") for dg in range(4)] for tt in range(NTO)]

        P.op("sp", lambda e: e.dma_start(out=cbf[:, :], in_=cbf_d[:, :]), writes=[t_c], dsem="c0")
        P.op("sp", lambda e: e.dma_start(out=lsel[:, :], in_=lsel_d[:, :]), writes=[], dsem="c0")
        t_c.w = P.ops["sp"][-1]
        P.op("sp", lambda e: e.dma_start(out=maskb[:, :], in_=maskb_d[:, :]), writes=[t_maskb], dsem="c1")
        P.op("sp", lambda e: e.dma_start(out=gains[:, :], in_=gains_d[:, :]), writes=[t_gains], dsem="c2")
        P.op("sp", lambda e: e.dma_start(out=hsel[:, :], in_=hsel_d[:, :]), writes=[t_hsel], dsem="c3")
        P.op("dve", lambda e: e.tensor_scalar(gq_s[:, :], gains[:, 0:depth], SC, None, ALU.mult),
             reads=[t_gains], writes=[t_gq])

        ident = cbf[:, C_IDENT:C_IDENT + 128]
        tri = cbf[:, C_TRI:C_TRI + 128]
        ones = cbf[:, C_ONES:C_ONES + 128]
        odiv = cbf[:, C_ODIV:C_ODIV + 128]
        nega = cbf[:, C_NEGA:C_NEGA + 128]

        cc_n = [0]
        rr = {"f512a": 0, "f512b": 0, "mixslot": 0, "arg": 0, "wh": 0, "pacc": 0, "xo": 0, "xt": 0, "f512": 0, "b512": 0, "pB": 0, "ptr": 0}

        def next_rr(name, n):
            v = rr[name]
            rr[name] = (v + 1) % n
            return v

        def load_w(src_d, row0, nchunk):
            s = next_rr("wh", 3)
            dst = wh[s]
            ncol = nchunk * 256
            pieces = list(range(0, ncol, 2048))
            for pi, c0 in enumerate(pieces):
                c1 = min(ncol, c0 + 2048)
                P.op("pool", (lambda e, dst=dst, c0=c0, c1=c1: e.dma_start(
                    out=dst[:, c0 // 256:c1 // 256, :],
                    in_=src_d[row0:row0 + 128, c0:c1].rearrange("p (c j) -> p c j", j=256))),
                    writes=[t_wh[s]] if pi == 0 else [], dsem=f"wh{s}")
            t_wh[s].w = P.ops["pool"][-1]
            return s

        wseq = []
        for L_ in range(depth):
            order_ = [(g_, hs_) for g_, heads_ in enumerate(groups) for hs_ in heads_]
            for idx_, (g_, hs_) in enumerate(order_):
                row_ = (L_ * NH + hs_) * 2
                wseq.append((win_d, row_ * 128, 16))
                wseq.append((win_d, (row_ + 1) * 128, 16))
                if idx_ == len(groups[0]) and NG > 1:
                    for dg_ in range(4):
                        wseq.append((wout_d, ((L_ * NG + 0) * 4 + dg_) * 128, 2 * len(groups[0]) * 2))
            for dg_ in range(4):
                wseq.append((wout_d, ((L_ * NG + (NG - 1)) * 4 + dg_) * 128, 2 * len(groups[NG - 1]) * 2))
        wstate = {"issued": 0, "used": 0, "slots": {}}

        def w_ensure(k):
            k = min(k, len(wseq) - 1)
            while wstate["issued"] <= k:
                i = wstate["issued"]
                src_d, row0, nchunk = wseq[i]
                wstate["slots"][i] = load_w(src_d, row0, nchunk)
                wstate["issued"] += 1

        def w_use(expect):
            i = wstate["used"]
            assert wseq[i] == expect, (i, wseq[i], expect)
            w_ensure(i)
            wstate["used"] += 1
            return wstate["slots"][i]

        def mm(out, lhsT, rhs, start, stop, reads=(), writes=()):
            return P.op("pe", lambda e: e.matmul(out, lhsT, rhs, start=start, stop=stop), reads=reads, writes=writes)

        def tr(out, in_, reads=(), writes=()):
            return P.op("pe", lambda e: e.transpose(out, in_, ident), reads=reads, writes=writes)

        def act(out, in_, func, reads, writes, **kw):
            return P.op("act", lambda e: e.activation(out=out, in_=in_, func=func, **kw), reads=reads, writes=writes)

        def norm_phase(L):
            P.op("sp", lambda e: e.dma_start(out=gbc[:, :], in_=ng_d[L, :].partition_broadcast(128)),
                 writes=[t_gbc], dsem="gbc")
            for tt in range(NTO):
                s = next_rr("xt", 2)
                src = x_d if L == 0 else y_d
                rd = [] if L == 0 else list(t_y[tt])
                P.op("sp", lambda e, s=s, tt=tt, src=src: e.dma_start(out=xt[s][:, :], in_=src[tt * 128:(tt + 1) * 128, :]),
                     reads=rd, writes=t_xt[s], dsem=f"xt{s}")
                st = stat[s]
                act(bfb[0][:, :], xt[s][:, :], AF.Square, t_xt[s], t_bfb[0] + [t_stat[s]], accum_out=st[:, 0:1])
                act(st[:, 1:2], st[:, 0:1], AF.Ln, [t_stat[s]], [t_stat[s]], bias=float(EPS), scale=1.0 / D)
                act(st[:, 2:3], st[:, 1:2], AF.Exp, [t_stat[s]], [t_stat[s]], scale=-0.5)
                hb_i = 2 + (tt % 2)
                P.op("dve", lambda e, s=s, st=st, hb_i=hb_i: e.scalar_tensor_tensor(
                    out=bfb[hb_i][:, :], in0=xt[s][:, :], scalar=st[:, 2:3], in1=gbc[:, :],
                    op0=ALU.mult, op1=ALU.mult),
                    reads=t_xt[s] + [t_stat[s], t_gbc], writes=t_bfb[hb_i])
                for rnd in range(4):
                    h = next_rr("ptr", 2)
                    for c in range(4):
                        cc = rnd * 4 + c
                        tr(ptrs[h][:, c * 128:(c + 1) * 128], bfb[hb_i][:, cc * 128:(cc + 1) * 128],
                           reads=t_bfb[hb_i] + [t_c], writes=[t_ptr[h]])
                    src_v = ptrs[h][:, 0:512].rearrange("p (c j) -> p c j", j=128)
                    dst_v = hT[:, rnd * 4:(rnd + 1) * 4, tt * 128:(tt + 1) * 128]
                    if (tt // 4) % 2 == 0:
                        P.op("dve", lambda e, dst_v=dst_v, src_v=src_v: e.tensor_copy(out=dst_v, in_=src_v),
                             reads=[t_ptr[h]], writes=[t_hT[0][tt // 4]])
                    else:
                        act(dst_v, src_v, AF.Copy, [t_ptr[h]], [t_hT[0][tt // 4]])
                if tt % 4 == 3:
                    tq = tt // 4
                    P.op("sp", lambda e, tq=tq: e.dma_start(
                        out=hx_in[tq].ap().rearrange("(c p) t -> p c t", p=128),
                        in_=hT[:, :, tq * 512:(tq + 1) * 512]),
                        reads=[t_hT[0][tq]], writes=[t_hxin[tq]], dsem=f"hxo{tq}")
                    cname = f"cc{cc_n[0]}"
                    cc_n[0] += 1
                    P.op("pool", lambda e, tq=tq: e.collective_compute(
                        "AllGather", ALU.bypass, replica_groups=RG,
                        ins=[hx_in[tq].ap().opt()], outs=[hx_out[tq].ap().opt()]),
                        reads=[t_hxin[tq]], writes=[t_hxout[tq]], dsem=cname, cc=True)
            for tq in range(2):
                for r in range(2):
                    P.op("sp", lambda e, tq=tq, r=r: e.dma_start(
                        out=hT[:, :, r * TO + tq * 512: r * TO + (tq + 1) * 512],
                        in_=hx_out[tq].ap()[r * D:(r + 1) * D, :].rearrange("(c p) t -> p c t", p=128)),
                        reads=[t_hxout[tq]], writes=[t_hT[r][tq]], dsem=f"hxl{r}{tq}")

        def in_proj(L, hs, is_b, tg_major=False):
            pend = []
            vpend = []

            def flush_pend():
                while pend:
                    pend.pop(0)()
            row = (L * NH + hs) * 2
            s_qk = w_use((win_d, row * 128, 16))
            s_vg = w_use((win_d, (row + 1) * 128, 16))
            if tg_major:
                order_ct = [(comp, tg) for tg in (0, 2, 1, 3) for comp in range(4)]
            else:
                order_ct = [(comp, tg) for comp in range(4) for tg in range(4)]
            for comp, tg in order_ct:
                if True:
                    s = s_qk if comp < 2 else s_vg
                    co = (comp % 2) * 128
                    a = next_rr("pacc", 3)
                    for c in range(DC):
                        mm(pacc[a][:, :], wh[s][:, c, co:co + 128], hT[:, c, tg * 512:(tg + 1) * 512],
                           c == 0, c == DC - 1,
                           reads=[t_wh[s], t_hT[tg // 2][tg % 2]],
                           writes=[t_pacc[a]])
                    cs = slice(tg * 512, (tg + 1) * 512)
                    if comp >= 2:
                        flush_pend()
                    while vpend:
                        vpend.pop(0)()
                    if comp == 2:
                        vT = bfb[1]
                        P.op("dve", lambda e, a=a, cs=cs: e.tensor_copy(out=bfb[1][:, cs], in_=pacc[a][:, :]),
                             reads=[t_pacc[a]], writes=[t_bfb[1][tg]])
                        def vtr(tg=tg):
                            h = next_rr("ptr", 2)
                            for b in range(4):
                                tb = tg * 4 + b
                                tr(ptrs[h][:, b * 128:(b + 1) * 128], bfb[1][:, tb * 128:(tb + 1) * 128],
                                   reads=[t_bfb[1][tg], t_c], writes=[t_ptr[h]])
                            act(Vt[:, tg * 4:(tg + 1) * 4, :],
                                ptrs[h][:, 0:512].rearrange("p (c j) -> p c j", j=128),
                                AF.Copy, [t_ptr[h]], [t_V[tg]])
                        vpend.append(vtr)
                    elif comp == 3:
                        f = next_rr("f512", 4)
                        act(f512[f][:, :], pacc[a][:, :], AF.Exp, [t_pacc[a]], [t_f512[f]], scale=-1.0)
                        act(f512[f][:, :], f512[f][:, :], AF.Ln, [t_f512[f]], [t_f512[f]], bias=1.0)
                        act(f512[f][:, :], f512[f][:, :], AF.Exp, [t_f512[f]], [t_f512[f]], scale=-1.0)
                        P.op("dve", lambda e, a=a, f=f, cs=cs: e.tensor_tensor(
                            out=sgT[:, cs], in0=pacc[a][:, :], in1=f512[f][:, :], op=ALU.mult),
                            reads=[t_pacc[a], t_f512[f]], writes=[t_sg[tg]])
                    elif not is_b:
                        if comp == 0:
                            P.op("dve", lambda e, a=a, cs=cs: e.tensor_scalar(qT[:, cs], pacc[a][:, :], SC, None, ALU.mult),
                                 reads=[t_pacc[a]], writes=[t_q[tg]])
                        else:
                            act(kT[:, cs], pacc[a][:, :], AF.Copy, [t_pacc[a]], [t_k[tg]])
                    else:
                        fr = next_rr("f512a", 2)
                        P.op("dve", lambda e, a=a, fr=fr: e.tensor_copy(out=f512[fr][:, :], in_=pacc[a][:, :]),
                             reads=[t_pacc[a]], writes=[t_f512[fr]])
                        bq = next_rr("b512", 2)
                        act(b512[bq][:, :], f512[fr][:, :], AF.Square, [t_f512[fr]], [t_b512[bq]])
                        def post(fr=fr, bq=bq, comp=comp, cs=cs, tg=tg):
                            u = next_rr("pB", 2)
                            mm(pB[u][:, :], odiv, b512[bq][:, :], True, True, reads=[t_b512[bq], t_c], writes=[t_pB[u]])
                            f2 = 2 + next_rr("f512b", 2)
                            act(f512[f2][:, :], pB[u][:, :], AF.Ln, [t_pB[u]], [t_f512[f2]], bias=float(EPS))
                            act(f512[f2][:, :], f512[f2][:, :], AF.Exp, [t_f512[f2]], [t_f512[f2]], scale=-0.5)
                            dst = qT if comp == 0 else kT
                            tdst = t_q if comp == 0 else t_k
                            gsc = gq_s[:, L:L + 1] if comp == 0 else gains[:, depth + L:depth + L + 1]
                            P.op("dve", lambda e: e.scalar_tensor_tensor(
                                out=dst[:, cs], in0=f512[fr][:, :], scalar=gsc, in1=f512[f2][:, :],
                                op0=ALU.mult, op1=ALU.mult),
                                reads=[t_f512[fr], t_f512[f2], t_gq, t_gains], writes=[tdst[tg]])
                        flush_pend()
                        pend.append(post)
            while vpend:
                vpend.pop(0)()

        def attn_a(ms):
            def blk(i):
                return slice(i * 128, (i + 1) * 128)

            def q512(I):
                return slice(I * 512, (I + 1) * 512)

            def sp_ap(j):
                return bfb[j // 4][:, (j % 4) * 512:(j % 4 + 1) * 512]

            def sp_tile(j):
                return t_bfb[j // 4][j % 4]

            def c0_of(I, j):
                return max(0, j - 4 * I) * 128

            def zstep(I, j):
                diag = j >= 4 * I
                c0 = c0_of(I, j)
                z = next_rr("pB", 2)
                mm(pB[z][:, c0:512], kT[:, blk(j)], qT[:, I * 512 + c0:(I + 1) * 512], True, not diag,
                   reads=[t_k[j // 4], t_q[I]], writes=[t_pB[z]])
                if diag:
                    jd = j - 4 * I
                    mm(pB[z][:, c0:512], ident, cbf[:, C_NEGW + jd * 512 + c0:C_NEGW + (jd + 1) * 512], False, True,
                       reads=[t_c], writes=[t_pB[z]])
                f = next_rr("f512", 4)
                act(f512[f][:, c0:512], pB[z][:, c0:512], AF.Exp, [t_pB[z]], [t_f512[f]])

                def z2():
                    act(sp_ap(j)[:, c0:512], f512[f][:, c0:512], AF.Ln, [t_f512[f]], [sp_tile(j)], bias=1.0)
                return z2

            def srow_step(I):
                u = I % 2
                nkb = 4 * I + 4
                for j in range(nkb):
                    c0 = c0_of(I, j)
                    mm(pacc[2][0:48, c0:512], cbf[:, C_EP + j * 48:C_EP + (j + 1) * 48], sp_ap(j)[:, c0:512],
                       j == 0, j == nkb - 1, reads=[sp_tile(j), t_c], writes=[t_pacc[2]])
                P.op("dve", lambda e: e.tensor_copy(out=srow[u][0:48, :], in_=pacc[2][0:48, :]),
                     reads=[t_pacc[2]], writes=[t_srow[u]])
                P.op("dve", lambda e: e.tensor_tensor(out=srow[u][32:48, :], in0=pacc[2][32:48, :],
                                                      in1=srow[u][32:48, :], op=ALU.subtract),
                     reads=[t_pacc[2], t_srow[u]], writes=[t_srow[u]])

            pend = []
            pend_w = []

            def astep(I, j):
                u = I % 2
                nkb = 4 * I + 4
                diag = j >= 4 * I
                last = (j == nkb - 1)
                a = next_rr("arg", 2)
                c0 = c0_of(I, j)
                o = pacc[a][:, c0:512]
                mm(o, kT[:, blk(j)], qT[:, I * 512 + c0:(I + 1) * 512], True, False,
                   reads=[t_k[j // 4], t_q[I]], writes=[t_pacc[a]])
                mm(o, tri, sp_ap(j)[:, c0:512], False, False, reads=[sp_tile(j), t_c], writes=[t_pacc[a]])
                if not last:
                    mm(o, lsel[0:48, blk(j)], srow[u][0:48, c0:512], False, not diag,
                       reads=[t_srow[u], t_c], writes=[t_pacc[a]])
                if diag:
                    jd = j - 4 * I
                    mm(o, ident, cbf[:, C_NEGW + jd * 512 + c0:C_NEGW + (jd + 1) * 512], False, True,
                       reads=[t_c], writes=[t_pacc[a]])
                bq = next_rr("b512", 2)

                def expw():
                    act(b512[bq][:, c0:512], pacc[a][:, c0:512], AF.Exp, [t_pacc[a]], [t_b512[bq]])

                def wv():
                    mm(psm[:, c0:512], Vt[:, j, :], b512[bq][:, c0:512], j == 0, j == nkb - 1,
                       reads=[t_V[j // 4], t_b512[bq]], writes=[t_psm])
                while pend:
                    pend.pop(0)()
                while pend_w:
                    ew, v = pend_w.pop(0)
                    ew()
                    pend.append(v)
                pend_w.append((expw, wv))

            NI = NT // 4
            zq = []
            for j in range(4):
                zq.append(zstep(0, j))
                if len(zq) > 1:
                    zq.pop(0)()
            while zq:
                zq.pop(0)()
            srow_step(0)
            for I in range(NI):
                nkb = 4 * I + 4
                for j in range(nkb):
                    z2 = zstep(I + 1, j) if I + 1 < NI else None
                    astep(I, j)
                    if z2 is not None:
                        z2()
                while pend:
                    pend.pop(0)()
                while pend_w:
                    ew, v = pend_w.pop(0)
                    ew()
                    v()
                P.op("dve", lambda e, I=I: e.tensor_tensor(out=mixT[:, ms, q512(I)], in0=psm[:, :], in1=sgT[:, q512(I)],
                                                          op=ALU.mult),
                     reads=[t_psm, t_sg[I]], writes=t_mix[ms][4 * I:4 * I + 4])
                if I + 1 < NI:
                    for j in range(nkb, nkb + 4):
                        zq.append(zstep(I + 1, j))
                        if len(zq) > 1:
                            zq.pop(0)()
                    while zq:
                        zq.pop(0)()
                    srow_step(I + 1)

        bm_done = set()

        def prep_bm(L, hb):
            bmi = 2 + hb % 2
            bm, t_bm = bfb[bmi], t_bfb[bmi][0:2]
            r0 = (L * n_hb + hb) * 128
            P.op("sp", lambda e: e.dma_start(out=bstage[:, :], in_=bias_d[r0:r0 + 128, :]), writes=[t_bstage], dsem="bst")
            P.op("dve", lambda e: e.tensor_tensor(out=bm[:, 0:640], in0=bstage[:, :], in1=maskb[:, :], op=ALU.add),
                 reads=[t_bstage, t_maskb], writes=t_bm)
            bm_done.add((L, hb))

        def attn_b(L, ms, hb):
            def blk(i):
                return slice(i * 128, (i + 1) * 128)

            bmi = 2 + hb % 2
            bm, t_bm = bfb[bmi], t_bfb[bmi][0:2]
            if (L, hb) not in bm_done:
                prep_bm(L, hb)

            def pslot(m):
                k = m % 3
                if k < 2:
                    return bfb[0], k * 1024, t_bfb[0][k * 2:k * 2 + 2]
                return bfb[1], 0, t_bfb[1][0:2]

            def sset(m):
                k = m % 3
                if k < 2:
                    return pB[k], t_pB[k], pacc[k], t_pacc[k]
                return pT[0], t_ptr[0], pT[1], t_ptr[1]

            def scores(m):
                jmin = max(0, m - 4)
                js = list(range(jmin, m + 1))
                bk, t_bk, ov, t_ov = sset(m)
                for jj, j in enumerate(js):
                    if jj < 4:
                        o, tw = bk[:, blk(jj)], t_bk
                    else:
                        o, tw = ov[:, 0:128], t_ov
                    mm(o, kT[:, blk(j)], qT[:, blk(m)], True, False, reads=[t_k[j // 4], t_q[m // 4]], writes=[tw])
                    mm(o, ident, bm[:, blk(m - j)], False, True, reads=t_bm + [t_c], writes=[tw])
                pb, base, tp = pslot(m)
                n4 = min(len(js), 4) * 128
                act(pb[:, base:base + n4], bk[:, 0:n4], AF.Exp, [t_bk], tp)
                if len(js) == 5:
                    act(pb[:, base + 512:base + 640], ov[:, 0:128], AF.Exp, [t_ov], tp)

            def dennum(m):
                jmin = max(0, m - 4)
                js = list(range(jmin, m + 1))
                pb, base, tp = pslot(m)
                rd = m % 2
                dn, t_dn = ((pacc[2], t_pacc[2]), (psm, t_psm))[(m // 2) % 2]
                for jj, j in enumerate(js):
                    mm(dn[:, blk(rd)], ones, pb[:, base + jj * 128:base + (jj + 1) * 128],
                       jj == 0, jj == len(js) - 1, reads=tp + [t_c], writes=[t_dn])
                for jj, j in enumerate(js):
                    mm(dn[:, blk(2 + rd)], Vt[:, j, :], pb[:, base + jj * 128:base + (jj + 1) * 128],
                       jj == 0, jj == len(js) - 1, reads=tp + [t_V[j // 4]], writes=[t_dn])

            def epi(mp):
                cs = slice(mp * 256, (mp + 1) * 256)
                f1 = next_rr("f512", 4)
                dn, t_dn = ((pacc[2], t_pacc[2]), (psm, t_psm))[mp % 2]
                act(f512[f1][:, 0:256], dn[:, 0:256], AF.Ln, [t_dn], [t_f512[f1]])
                f2 = next_rr("f512", 4)
                epi2q.append(lambda: epi_b(mp, cs, f1, f2, dn, t_dn))

            def epi_b(mp, cs, f1, f2, dn, t_dn):
                act(f512[f1][:, 0:256], f512[f1][:, 0:256], AF.Exp, [t_f512[f1]], [t_f512[f1]], scale=-1.0)
                P.op("dve", lambda e: e.tensor_tensor(out=f512[f2][:, 0:256], in0=dn[:, 256:512], in1=f512[f1][:, 0:256],
                                                      op=ALU.mult),
                     reads=[t_dn, t_f512[f1]], writes=[t_f512[f2]])
                P.op("dve", lambda e: e.tensor_tensor(out=mixT[:, ms, cs], in0=f512[f2][:, 0:256], in1=sgT[:, cs],
                                                      op=ALU.mult),
                     reads=[t_f512[f2], t_sg[mp // 2]], writes=[t_mix[ms][2 * mp], t_mix[ms][2 * mp + 1]])

            epi2q = []
            for n in range(NT + 3):
                if n < NT:
                    scores(n)
                while epi2q:
                    epi2q.pop(0)()
                if 0 <= n - 2 < NT:
                    dennum(n - 2)
                    if (n - 2) % 2 == 1:
                        epi((n - 2) // 2)

        def out_proj(L, g, first):
            nh = 2 * len(groups[g])
            steps = [(dg, tt) for dg in range(4) for tt in range(NTO)]
            slots = {}

            def load(k):
                dg, tt = steps[k]
                so = next_rr("xo", NXO)
                slots[k] = so
                src = x_d if first else y_d
                P.op("sp", lambda e: e.dma_start(out=xo[so], in_=src[tt * 128:(tt + 1) * 128, dg * 512:(dg + 1) * 512]),
                     reads=[] if first else [t_y[tt][dg]], writes=[t_xo[so]], dsem=f"xo{so}")

            PF = 5
            for k0 in range(PF):
                load(k0)
            s = None
            for k, (dg, tt) in enumerate(steps):
                if tt == 0:
                    s = w_use((wout_d, ((L * NG + g) * 4 + dg) * 128, nh * 2))
                    if dg >= 1:
                        w_ensure(wstate["used"])
                if k + PF < len(steps):
                    load(k + PF)
                a = next_rr("pacc", 3)
                for hh in range(nh):
                    mm(pacc[a][:, :], mixSel[:, hh, tt * 128:(tt + 1) * 128],
                       wh[s][:, 2 * hh:2 * hh + 2, :].rearrange("p a b -> p (a b)"),
                       hh == 0, hh == nh - 1, reads=[t_sel[hh], t_wh[s]], writes=[t_pacc[a]])
                so = slots[k]
                P.op("dve", lambda e, so=so, a=a: e.tensor_tensor(out=xo[so], in0=xo[so], in1=pacc[a][:, :],
                                                                  op=ALU.add),
                     reads=[t_xo[so], t_pacc[a]], writes=[t_xo[so]])
                P.op("sp", lambda e, so=so, tt=tt, dg=dg: e.dma_start(
                    out=y_d[tt * 128:(tt + 1) * 128, dg * 512:(dg + 1) * 512], in_=xo[so]),
                    reads=[t_xo[so]], writes=[t_y[tt][dg]], dsem=f"ys{so}")

        def exchange(ms, hh, nhg):
            P.op("sp", lambda e: e.dma_start(out=mx_in[ms].ap(), in_=mixT[:, ms, :]),
                 reads=t_mix[ms], writes=[t_mxin[ms]], dsem=f"mxo{ms}")
            cname = f"cc{cc_n[0]}"
            cc_n[0] += 1
            P.op("pool", lambda e: e.collective_compute(
                "AllGather", ALU.bypass, replica_groups=RG,
                ins=[mx_in[ms].ap().opt()], outs=[mx_out[ms].ap().opt()]),
                reads=[t_mxin[ms]], writes=[t_mxout[ms]], dsem=cname, cc=True)

        def exchange_recv(ms, hh, nhg):
            P.op("sp", lambda e: e.dma_start(out=stg_lo[:, :, :],
                                             in_=mx_out[ms].ap()[:, 0:TO].rearrange("(r p) t -> p r t", p=128)),
                 reads=[t_mxout[ms]], writes=[t_stg], dsem="stg")
            P.op("sp", lambda e: e.dma_start(out=stg_hi[:, :, :],
                                             in_=mx_out[ms].ap()[:, TO:T].rearrange("(r p) t -> p r t", p=128)),
                 reads=[t_mxout[ms]], writes=[], dsem="stg")
            t_stg.w = P.ops["sp"][-1]
            P.op("dve", lambda e: e.tensor_scalar(stg_hi[:, :, :], stg_hi[:, :, :], hsel[:, 1:2], None, ALU.mult),
                 reads=[t_stg, t_hsel], writes=[t_stg])
            for r in range(2):
                hidx = r * nhg + hh
                P.op("dve", lambda e, r=r, hidx=hidx: e.scalar_tensor_tensor(
                    out=mixSel[:, hidx, :], in0=stg_lo[:, r, :], scalar=hsel[:, 0:1], in1=stg_hi[:, r, :],
                    op0=ALU.mult, op1=ALU.add),
                    reads=[t_stg, t_hsel], writes=[t_sel[hidx]])

        for L in range(depth):
            if dbg >= 1:
                P.phase = "norm"
                norm_phase(L)
            pending_op = None
            pending_rx = []

            def flush_rx():
                P.phase = "exch"
                while pending_rx:
                    exchange_recv(*pending_rx.pop(0))

            for g, heads in enumerate(groups):
                for hh, hs in enumerate(heads):
                    is_b = hs >= n_ha
                    ms = next_rr("mixslot", 2)
                    if dbg >= 2:
                        P.phase = "inproj_b" if is_b else "inproj_a"
                        if is_b and hs - n_ha >= 1:
                            prep_bm(L, hs - n_ha)
                        in_proj(L, hs, is_b, tg_major=(g == 0 and hh == 0 and not is_b))
                        w_ensure(wstate["used"] + 1)
                    flush_rx()
                    if pending_op is not None:
                        P.phase = "outproj"
                        out_proj(L, pending_op, first=(L == 0 and pending_op == 0))
                        pending_op = None
                    if dbg >= 3:
                        P.phase = "attn_b" if is_b else "attn_a"
                        if is_b:
                            attn_b(L, ms, hs - n_ha)
                        else:
                            attn_a(ms)
                        P.phase = "exch"
                        exchange(ms, hh, len(heads))
                        pending_rx.append((ms, hh, len(heads)))
                if dbg >= 4:
                    if g + 1 < len(groups):
                        pending_op = g
                    else:
                        flush_rx()
                        P.phase = "outproj"
                        out_proj(L, g, first=(L == 0 and g == 0))

        dnames = sorted(P.dma_cnt.keys())
        sem_objs = {}
        for e in Prog.ENGS:
            sem_objs[e] = es.enter_context(nc.semaphore(f"s_{e}"))
        dsem_objs = {}
        for dn in dnames:
            dsem_objs[dn] = es.enter_context(nc.semaphore(f"d_{dn}"))
        block = es.enter_context(nc.Block())
        P.emit(block, sem_objs, dsem_objs)
    nc._prog_stats = {e: (len(P.ops[e]), P.sig_counts[e], P.nwaits[e]) for e in Prog.ENGS}
    nc._pe_phases = [o.ph for o in P.ops["pe"]]
    return nc


def prep_inputs(x, norm_g, w_in, q_norm_g, k_norm_g, rel_bias, w_out, depth=DEPTH, n_ha=NHA_C, n_hb=NHB_C,
                n_cores=N_CORES):
    NH = n_ha + n_hb
    cbf, lsel, maskb = host_consts()
    w_in = np.asarray(w_in, np.float32)
    w_out = np.asarray(w_out, np.float32)
    xs = np.asarray(x, np.float32)
    rb_all = np.asarray(rel_bias, np.float32)
    groups = []
    if n_ha:
        groups.append(("a", n_ha))
    if n_hb:
        groups.append(("b", n_hb))
    GMAX = max(n for _, n in groups)
    wout = np.zeros((depth, len(groups), 4, 128, 2 * GMAX, 512), np.float32)
    for gi, (kind, n) in enumerate(groups):
        for hidx in range(2 * n):
            e0 = hidx * 128 if kind == "a" else 1024 + hidx * 128
            for dg in range(4):
                wout[:, gi, dg, :, hidx, :] = w_out[:depth, e0:e0 + 128, dg * 512:(dg + 1) * 512]
    wout = wout.reshape(depth * len(groups) * 4 * 128, 2 * GMAX * 512)
    gains = np.ascontiguousarray(np.concatenate(
        [np.asarray(q_norm_g, np.float32)[:depth].T, np.asarray(k_norm_g, np.float32)[:depth].T], axis=1))
    s_i = np.arange(128)[:, None, None]
    dl = np.arange(5)[None, :, None]
    t_i = np.arange(128)[None, None, :]
    idx = np.clip(dl * 128 + t_i - s_i, -63, 256) + 63
    shared = {"wout": wout, "ng": np.ascontiguousarray(np.asarray(norm_g, np.float32)[:depth]),
              "gains": gains, "maskb": maskb, "cbf": cbf, "lsel": lsel}
    per_half = []
    for g in range(2):
        win = np.empty((depth, NH, 2, 128, 16, 256), np.float32)
        for hs in range(NH):
            if hs < n_ha:
                base, h = 0, g * n_ha + hs
            else:
                base, h = 4096, g * n_hb + (hs - n_ha)
            for comp in range(4):
                col0 = base + comp * 1024 + h * 128
                blk = w_in[:depth, :, col0:col0 + 128].reshape(depth, 16, 128, 128)
                win[:, hs, comp // 2, :, :, (comp % 2) * 128:(comp % 2) * 128 + 128] = blk.transpose(0, 2, 1, 3)
        win = win.reshape(depth * NH * 2 * 128, 16 * 256)
        nb = max(n_hb, 1)
        rb = rb_all[:depth, g * n_hb:g * n_hb + nb]
        biasT = np.ascontiguousarray(rb[:, :, idx]).reshape(depth * nb * 128, 640)
        hsel = np.zeros((128, 2), np.float32)
        hsel[:, g] = 1.0
        per_half.append({"win": win, "biasT": biasT, "hsel": hsel})
    in_maps = []
    for c in range(n_cores):
        b, g = c // 2, c % 2
        m = dict(shared)
        m.update(per_half[g])
        m["x"] = np.ascontiguousarray(xs[b % xs.shape[0], g * TO:(g + 1) * TO])
        in_maps.append(m)
    return in_maps


_NC_CACHE = {}


def kernel(x, norm_g, w_in, q_norm_g, k_norm_g, rel_bias, w_out):
    if "full" not in _NC_CACHE:
        _NC_CACHE["full"] = build_program()
    nc = _NC_CACHE["full"]
    in_maps = prep_inputs(x, norm_g, w_in, q_norm_g, k_norm_g, rel_bias, w_out)
    res = run_bass_kernel_spmd(nc, in_maps, core_ids=list(range(N_CORES)))
    B = np.asarray(x).shape[0]
    out = np.empty((B, T, D), np.float32)
    for c in range(N_CORES):
        b, g = c // 2, c % 2
        out[b, g * TO:(g + 1) * TO] = np.asarray(res.results[c]["y"], np.float32)
    return out
```
